# Optimizing a Trainium2 kernel written in Bass

```python
import numpy as np
import jax
import jax.numpy as jnp
from jax import lax

D_MODEL = 1024
BATCH = 1
SEQ = 16384
DEPTH = 4

GRID_W = 64
CTX_LEN = 256
N_MIXERS = 2
N_HEADS = 16
HEAD_DIM = D_MODEL // N_HEADS
D_FF = 2816
CONV_WIDTH = 31
WIN_R = 8
WIN_C = 16
N_MOD = 9
N_CONV_LAYERS = (DEPTH + 1) // 2
N_NA_LAYERS = DEPTH // 2
RMS_EPS = 1e-6
LN_EPS = 1e-5
MASK_VALUE = -1e30

kernel_name = 'hybrid_conformer_natten_dit_trunk'


def _rmsnorm(x, g):
    xf = x.astype(jnp.float32)
    y = xf * lax.rsqrt(jnp.mean(xf * xf, axis=-1, keepdims=True) + RMS_EPS)
    return y.astype(x.dtype) * g


def _layernorm(x, g, b):
    xf = x.astype(jnp.float32)
    mu = jnp.mean(xf, axis=-1, keepdims=True)
    var = jnp.mean(jnp.square(xf - mu), axis=-1, keepdims=True)
    return ((xf - mu) * lax.rsqrt(var + LN_EPS)).astype(x.dtype) * g + b


def _modulate(h, shift, scale):
    return h * (1 + scale) + shift


def _swiglu(h, wg, wu, wd):
    return (jax.nn.silu(h @ wg) * (h @ wu)) @ wd


def _ffn_half(s, mod, g, wg, wu, wd):
    shift, scale, gate = mod
    h = _modulate(_rmsnorm(s, g), shift, scale)
    return s + 0.5 * gate * _swiglu(h, wg, wu, wd)


def _conv_module(h, w1, b1, wdw, bdw, lng, lnb, w2, b2):
    a, gt = jnp.split(h @ w1 + b1, 2, axis=-1)
    u = a * jax.nn.sigmoid(gt)
    u = lax.conv_general_dilated(
        u, wdw[:, None, :], window_strides=(1,),
        padding=[(CONV_WIDTH // 2, CONV_WIDTH // 2)],
        dimension_numbers=('NWC', 'WIO', 'NWC'),
        feature_group_count=u.shape[-1]) + bdw
    u = jax.nn.silu(_layernorm(u, lng, lnb))
    return u @ w2 + b2


def _col_window_tables():
    ncb = GRID_W // WIN_C
    span = 2 * WIN_C
    n = np.arange(ncb)
    blk_start = np.clip(n * WIN_C - WIN_C // 2, 0, GRID_W - span)
    key_cols = blk_start[:, None] + np.arange(span)[None, :]
    q_cols = n[:, None] * WIN_C + np.arange(WIN_C)[None, :]
    q_start = np.clip(q_cols - WIN_C // 2, 0, GRID_W - WIN_C)
    kc3 = key_cols[:, None, :]
    valid = (kc3 >= q_start[:, :, None]) & (kc3 < q_start[:, :, None] + WIN_C)
    rel = np.clip(kc3 - q_cols[:, :, None], -(WIN_C - 1), WIN_C - 1) + WIN_C - 1
    return key_cols.astype(np.int32), valid, rel.astype(np.int32)


def _neighbourhood_attention(q, k, v, kc, vc, rpb):
    B, T, H, Dh = q.shape
    rows = T // GRID_W
    wr = min(WIN_R, rows)
    key_cols, valid, rel_idx = _col_window_tables()
    ncb, span = key_cols.shape
    key_cols_j = jnp.asarray(key_cols)
    rel_idx_j = jnp.asarray(rel_idx)
    valid_j = jnp.asarray(valid)[None, None, :, :, None, :]
    rpb32 = rpb.astype(jnp.float32)
    qg = (q * (Dh ** -0.5)).reshape(B, rows, ncb, WIN_C, H, Dh).transpose(1, 0, 2, 3, 4, 5)
    kg = k.reshape(B, rows, GRID_W, H, Dh)
    vg = v.reshape(B, rows, GRID_W, H, Dh)

    def row_step(args):
        r, q_row = args
        r0 = jnp.clip(r - wr // 2, 0, rows - wr)
        k_band = lax.dynamic_slice_in_dim(kg, r0, wr, axis=1)
        v_band = lax.dynamic_slice_in_dim(vg, r0, wr, axis=1)
        k_blk = jnp.take(k_band, key_cols_j, axis=2)
        v_blk = jnp.take(v_band, key_cols_j, axis=2)
        s_lat = jnp.einsum('bnqhd,bjnmhd->bhnqjm', q_row, k_blk).astype(jnp.float32)
        rel_r = r0 + jnp.arange(wr) - r + WIN_R - 1
        bias = jnp.take(rpb32, rel_r, axis=1)
        bias = jnp.take(bias, rel_idx_j, axis=2).transpose(0, 2, 3, 1, 4)
        s_lat = jnp.where(valid_j, s_lat + bias, MASK_VALUE)
        s_ctx = jnp.einsum('bnqhd,bkhd->bhnqk', q_row, kc).astype(jnp.float32)
        s = jnp.concatenate([s_lat.reshape(B, H, ncb, WIN_C, wr * span), s_ctx], axis=-1)
        p = jax.nn.softmax(s, axis=-1).astype(v.dtype)
        p_lat = p[..., :wr * span].reshape(B, H, ncb, WIN_C, wr, span)
        p_ctx = p[..., wr * span:]
        return (jnp.einsum('bhnqjm,bjnmhd->bnqhd', p_lat, v_blk)
                + jnp.einsum('bhnqk,bkhd->bnqhd', p_ctx, vc))

    o = lax.map(row_step, (jnp.arange(rows), qg))
    return o.transpose(1, 0, 2, 3, 4, 5).reshape(B, T, H * Dh)


def _context_attention(qc, kc, vc):
    B, K, H, Dh = qc.shape
    s = jnp.einsum('bqhd,bkhd->bhqk', qc * (Dh ** -0.5), kc).astype(jnp.float32)
    p = jax.nn.softmax(s, axis=-1).astype(vc.dtype)
    return jnp.einsum('bhqk,bkhd->bqhd', p, vc).reshape(B, K, H * Dh)


def setup_inputs(seed: int = 0) -> dict:
    key = jax.random.key(seed)
    ks = iter(jax.random.split(key, 32))

    def nrm(shape, scale):
        return jax.random.normal(next(ks), shape, jnp.float32) * scale

    D, F = D_MODEL, D_FF
    return {
        'x': nrm((BATCH, SEQ, D), 1.0),
        'c': nrm((BATCH, D), 1.0),
        'ctx': nrm((BATCH, CTX_LEN, D), 1.0),
        'c_ctx': nrm((D,), 1.0),
        'mod_w': nrm((DEPTH, D, N_MOD * D), 0.5 * D ** -0.5),
        'mod_b': nrm((DEPTH, N_MOD * D), 0.02),
        'norm_g': 1.0 + nrm((DEPTH, 3, D), 0.05),
        'ffn_w_gate': nrm((DEPTH, 2, D, F), D ** -0.5),
        'ffn_w_up': nrm((DEPTH, 2, D, F), D ** -0.5),
        'ffn_w_down': nrm((DEPTH, 2, F, D), F ** -0.5),
        'conv_w_pw1': nrm((N_CONV_LAYERS, D, 2 * D), D ** -0.5),
        'conv_b_pw1': nrm((N_CONV_LAYERS, 2 * D), 0.02),
        'conv_w_dw': nrm((N_CONV_LAYERS, CONV_WIDTH, D), CONV_WIDTH ** -0.5),
        'conv_b_dw': nrm((N_CONV_LAYERS, D), 0.02),
        'conv_ln_g': 1.0 + nrm((N_CONV_LAYERS, D), 0.05),
        'conv_ln_b': nrm((N_CONV_LAYERS, D), 0.02),
        'conv_w_pw2': nrm((N_CONV_LAYERS, D, D), D ** -0.5),
        'conv_b_pw2': nrm((N_CONV_LAYERS, D), 0.02),
        'na_w_qkv': nrm((N_NA_LAYERS, D, 3 * D), D ** -0.5),
        'na_b_qkv': nrm((N_NA_LAYERS, 3 * D), 0.02),
        'na_rpb': nrm((N_NA_LAYERS, N_HEADS, 2 * WIN_R - 1, 2 * WIN_C - 1), 0.1),
        'na_w_o': nrm((N_NA_LAYERS, D, D), D ** -0.5),
        'na_b_o': nrm((N_NA_LAYERS, D), 0.02),
        'final_g': 1.0 + nrm((D,), 0.05),
    }


def reference(x, c, ctx, c_ctx, mod_w, mod_b, norm_g, ffn_w_gate, ffn_w_up, ffn_w_down,
              conv_w_pw1, conv_b_pw1, conv_w_dw, conv_b_dw, conv_ln_g, conv_ln_b, conv_w_pw2, conv_b_pw2,
              na_w_qkv, na_b_qkv, na_rpb, na_w_o, na_b_o, final_g):
    B, T, D = x.shape
    K = ctx.shape[1]
    xc = ctx
    for i in range(DEPTH):
        mixer = i % N_MIXERS
        j = i // N_MIXERS
        last = i == DEPTH - 1
        run_ctx = (not last) or mixer == 1
        ml = (jax.nn.silu(c) @ mod_w[i] + mod_b[i]).reshape(B, 1, N_MOD, D)
        mc = (jax.nn.silu(c_ctx) @ mod_w[i] + mod_b[i]).reshape(N_MOD, D)
        lat_m = [ml[:, :, n] for n in range(N_MOD)]
        ctx_m = [mc[n] for n in range(N_MOD)]

        x = _ffn_half(x, lat_m[0:3], norm_g[i, 0], ffn_w_gate[i, 0], ffn_w_up[i, 0], ffn_w_down[i, 0])
        if run_ctx:
            xc = _ffn_half(xc, ctx_m[0:3], norm_g[i, 0], ffn_w_gate[i, 0], ffn_w_up[i, 0], ffn_w_down[i, 0])

        h = _modulate(_rmsnorm(x, norm_g[i, 1]), lat_m[3], lat_m[4])
        if run_ctx:
            hc = _modulate(_rmsnorm(xc, norm_g[i, 1]), ctx_m[3], ctx_m[4])
        if mixer == 0:
            conv_p = (conv_w_pw1[j], conv_b_pw1[j], conv_w_dw[j], conv_b_dw[j],
                      conv_ln_g[j], conv_ln_b[j], conv_w_pw2[j], conv_b_pw2[j])
            y = _conv_module(h, *conv_p)
            if not last:
                yc = _conv_module(hc, *conv_p)
        else:
            qkv = (h @ na_w_qkv[j] + na_b_qkv[j]).reshape(B, T, 3, N_HEADS, HEAD_DIM)
            qkvc = (hc @ na_w_qkv[j] + na_b_qkv[j]).reshape(B, K, 3, N_HEADS, HEAD_DIM)
            kc, vc = qkvc[:, :, 1], qkvc[:, :, 2]
            o = _neighbourhood_attention(qkv[:, :, 0], qkv[:, :, 1], qkv[:, :, 2], kc, vc, na_rpb[j])
            y = o @ na_w_o[j] + na_b_o[j]
            if not last:
                yc = _context_attention(qkvc[:, :, 0], kc, vc) @ na_w_o[j] + na_b_o[j]
        x = x + lat_m[5] * y

        x = _ffn_half(x, lat_m[6:9], norm_g[i, 2], ffn_w_gate[i, 1], ffn_w_up[i, 1], ffn_w_down[i, 1])
        if not last:
            xc = xc + ctx_m[5] * yc
            xc = _ffn_half(xc, ctx_m[6:9], norm_g[i, 2], ffn_w_gate[i, 1], ffn_w_up[i, 1], ffn_w_down[i, 1])
    return _rmsnorm(x, final_g)
```

```python
import numpy as np
from contextlib import ExitStack
import concourse.bass as bass
import concourse.mybir as mybir
from concourse.bass_utils import run_bass_kernel_spmd

F32 = mybir.dt.float32
BF16 = mybir.dt.bfloat16
AF = mybir.ActivationFunctionType
ALU = mybir.AluOpType

NCORES = 8
D = 1024
KC = 8
FF = 2816
FC = 22
DEPTH = 4
SEQ = 16384
NOWN = SEQ // NCORES
NCTX = 256
NX = NOWN + NCTX
NHALO = 512
NEXT = NX + NHALO
N_HEADS = 16
CONV_W = 31
RMS_EPS = 1e-6
LN_EPS = 1e-5
NEG = -30000.0
MODSH = 9 * D // NCORES
UL = 16 + NOWN + 16 + 16 + NCTX + 16
CB = 16 + NOWN + 16


class T:
    __slots__ = ("name", "w", "r", "dsem", "dcnt")

    def __init__(self, name):
        self.name = name
        self.w = None
        self.r = {}
        self.dsem = None
        self.dcnt = 0


class Sync:
    def __init__(self, nc, es):
        self.nc = nc
        self.es = es
        self.eng = {"pe": nc.tensor, "act": nc.scalar, "dve": nc.vector, "pool": nc.gpsimd, "sp": nc.sync}
        self.sem = {k: es.enter_context(nc.semaphore("s_" + k)) for k in self.eng}
        self.cnt = {k: 0 for k in self.eng}
        self.seen = {k: {} for k in self.eng}
        self.nsem = 0

    def _wait(self, e, dep, raw=False):
        if dep is None:
            return
        key, sem, c = dep
        if key == e:
            if e == "pe" or not raw or c > self.cnt[e]:
                return
        if self.seen[e].get(key, 0) >= c:
            return
        if key in self.cnt:
            assert c <= self.cnt[key], (e, key, c, self.cnt[key])
        self.eng[e].wait_ge(sem, c)
        self.seen[e][key] = c

    def _deps(self, e, reads, writes):
        for t in reads:
            self._wait(e, t.w, raw=True)
        for t in writes:
            self._wait(e, t.w)
            for k, (sem, c) in t.r.items():
                self._wait(e, (k, sem, c))

    def op(self, e, fn, reads=(), writes=(), signal=True):
        self._deps(e, reads, writes)
        ins = fn(self.eng[e])
        if signal:
            self.cnt[e] += 1
            ins.then_inc(self.sem[e], 1)
            c = self.cnt[e]
        else:
            c = self.cnt[e] + 1
        for t in reads:
            t.r[e] = (self.sem[e], c)
        for t in writes:
            t.w = (e, self.sem[e], c)
            t.r = {}
        return ins

    def dsem_for(self, t):
        if t.dsem is None:
            self.nsem += 1
            t.dsem = self.es.enter_context(self.nc.semaphore("d%d_%s" % (self.nsem, t.name)))
        return t.dsem

    def dma(self, q, out, in_, reads=(), writes=(), sem_tile=None):
        self._deps(q, reads, writes)
        st = sem_tile or (writes[0] if writes else reads[0])
        sem = self.dsem_for(st)
        st.dcnt += 16
        self.eng[q].dma_start(out=out, in_=in_).then_inc(sem, 16)
        key = "dma_" + st.name
        for t in reads:
            t.r[key] = (sem, st.dcnt)
        for t in writes:
            t.w = (key, sem, st.dcnt)
            t.r = {}


def fm(v):
    v = np.asarray(v)
    return np.ascontiguousarray(v.reshape(-1, 128).T)


def _vec_layout():
    lay = {}
    off = 0
    for name, nl, width in (("norm_g", DEPTH, 3 * KC), ("final_g", 1, KC),
                            ("conv_b1", 2, 16), ("conv_wdw", 2, KC * CONV_W), ("conv_bdw", 2, KC),
                            ("conv_lng", 2, KC), ("conv_lnb", 2, KC), ("conv_b2", 2, KC),
                            ("na_bqkv", 2, 24), ("na_bo", 2, KC)):
        lay[name] = (off, width)
        off += nl * width
    return lay, off


VEC_LAYOUT, NVEC = _vec_layout()


class Builder:
    def __init__(self, stages, final_norm=False, mods_only=False):
        self.stages = stages
        self.final_norm = final_norm
        self.mods_only = mods_only
        self.nc = bass.Bass("TRN2", target_bir_lowering=False)
        self.es = ExitStack()
        self.s = Sync(self.nc, self.es)
        self.inputs = {}
        self._din = {}
        self._uid = 0

    def din(self, name, shape, dt=F32):
        if name not in self._din:
            t = self.nc.dram_tensor(name, list(shape), dt, kind="ExternalInput")
            self.inputs[name] = (tuple(shape), dt)
            self._din[name] = t.ap()
        return self._din[name]

    def dout(self, name, shape, dt=F32):
        return self.nc.dram_tensor(name, list(shape), dt, kind="ExternalOutput").ap()

    def sb(self, name, shape, dt, es=None):
        self._uid += 1
        return (es or self.es).enter_context(
            self.nc.sbuf_tensor("%s_%d" % (name, self._uid), list(shape), dt))

    def tile(self, name):
        self._uid += 1
        return T("%s%d" % (name, self._uid))

    def vcol(self, name, layer):
        off, width = VEC_LAYOUT[name]
        return self.vec[:, off + layer * width: off + (layer + 1) * width]

    def stage_barrier(self):
        s = self.s
        for e in ("pe", "act", "dve", "pool", "sp"):
            for k in ("pe", "act", "dve", "pool"):
                if k != e and s.cnt[k] > 0:
                    s._wait(e, (k, s.sem[k], s.cnt[k]))

    def setup_common(self):
        nc = self.nc
        self.psum = [self.es.enter_context(nc.psum_tensor("ps%d" % i, [128, 512], F32)) for i in range(8)]
        self.t_ps = [self.tile("ps") for _ in range(8)]

    def setup(self):
        nc, s = self.nc, self.s
        self.setup_common()
        d_xT = self.din("xT", [128, KC, NX])
        d_vec = self.din("vecs", [128, NVEC])
        d_mods = self.din("modsT", [128, DEPTH, 72, 2])
        d_ident = self.din("ident", [128, 128])
        self.d_out = self.dout("outT", [128, KC, NX])

        self.xT = self.sb("xT_sb", [128, KC, NX], F32)
        self.hT = self.sb("hT_sb", [128, KC, NEXT], BF16)
        self.vec = self.sb("vec_sb", [128, NVEC], F32)
        self.modall = self.sb("modall", [128, DEPTH, 72, 2], F32)
        self.mA = self.sb("mA_sb", [128, KC, 2], F32)
        self.mG = self.sb("mG_sb", [128, KC, 2], F32)
        self.mB = self.sb("mB_sb", [128, KC, 2], F32)
        self.ones = self.sb("ones_sb", [128, 128], BF16)
        self.ident = self.sb("ident_sb", [128, 128], BF16)
        self.eps_rms = self.sb("eps_rms", [128, 1], F32)
        self.eps_ln = self.sb("eps_ln", [128, 1], F32)
        self.sq = [self.sb("sq", [128, 512], BF16) for i in range(2)]
        self.tmp = [self.sb("tmp", [128, 512], F32) for i in range(2)]
        self.rstd = self.sb("rstd_sb", [128, 512], F32)

        self.t_x = [self.tile("x") for _ in range(5)]
        self.t_h = [self.tile("h") for _ in range(6)]
        self.t_vec = self.tile("vec")
        self.t_modall = self.tile("modall")
        self.t_mAG = self.tile("mAG")
        self.t_ones = self.tile("ones")
        self.t_eps = self.tile("eps")
        self.t_sq = [self.tile("sq") for _ in range(2)]
        self.t_tmp = [self.tile("tmp") for _ in range(2)]
        self.t_rstd = self.tile("rstd")
        self.t_out = self.tile("out")

        self.xtiles = [(i * 512, 512, 0) for i in range(4)] + [(NOWN, NCTX, 1)]

        for i, (o, n, _) in enumerate(self.xtiles):
            s.dma("sp", self.xT[:, :, o:o + n], d_xT[:, :, o:o + n], writes=[self.t_x[i]])
        s.dma("sp", self.vec[:], d_vec, writes=[self.t_vec])
        s.dma("sp", self.modall[:], d_mods, writes=[self.t_modall])
        s.dma("pool", self.ident[:], d_ident, writes=[self.t_ones])
        s.op("dve", lambda e: e.memset(self.ones[:], 1.0), writes=[self.t_ones])
        s.op("dve", lambda e: e.memset(self.eps_rms[:], RMS_EPS), writes=[self.t_eps])
        s.op("dve", lambda e: e.memset(self.eps_ln[:], LN_EPS), writes=[self.t_eps])

    def emit_mods_shard(self):
        s = self.s
        self.setup_common()
        d_w = self.din("modw_sh", [DEPTH, D, MODSH])
        d_b = self.din("modb_sh", [128, DEPTH, 9])
        d_cs = self.din("cs", [128, KC, 2])
        d_o = self.dout("mods_sh", [128, DEPTH, 9, 2])
        cs = self.sb("cs", [128, KC, 2], F32)
        scs = self.sb("scs", [128, KC, 2], BF16)
        mb = self.sb("mb", [128, DEPTH, 9], F32)
        res = self.sb("res", [128, DEPTH, 9, 2], F32)
        wbuf = [self.sb("modw", [128, KC, 384], BF16) for i in range(2)]
        t_cs, t_scs, t_mb, t_res = (self.tile(n) for n in ("cs", "scs", "mb", "res"))
        t_w = [self.tile("modw") for _ in range(2)]
        s.dma("sp", cs[:], d_cs, writes=[t_cs])
        s.dma("sp", mb[:], d_b, writes=[t_mb])
        s.op("act", lambda e: e.activation(out=scs[:], in_=cs[:], func=AF.Silu), reads=[t_cs], writes=[t_scs])
        ps, t_ps = self.psum[7], self.t_ps[7]
        n = 0
        for l in range(DEPTH):
            src = d_w[l].rearrange("(k p) f -> p k f", p=128)
            for pc in range(3):
                b = n % 2
                n += 1
                s.dma("pool", wbuf[b][:], src[:, :, pc * 384:(pc + 1) * 384], writes=[t_w[b]])
                for oc in range(3):
                    o = l * 9 + pc * 3 + oc
                    for k in range(KC):
                        s.op("pe", lambda e, b=b, oc=oc, k=k, o=o: e.matmul(
                            ps[:, 2 * o:2 * o + 2], wbuf[b][:, k, oc * 128:(oc + 1) * 128], scs[:, k, :],
                            start=(k == 0), stop=(k == KC - 1)),
                            reads=[t_w[b], t_scs], writes=[t_ps], signal=(k == KC - 1))
        psv = ps[:, 0:2 * DEPTH * 9].rearrange("p (l q t) -> p l q t", l=DEPTH, q=9)
        for col in range(2):
            s.op("dve", lambda e, col=col: e.tensor_tensor(out=res[:, :, :, col], in0=psv[:, :, :, col],
                                                           in1=mb[:], op=ALU.add),
                 reads=[t_ps, t_mb], writes=[t_res])
        s.dma("sp", d_o, res[:], reads=[t_res], sem_tile=t_res)
        s._wait("sp", ("dma_" + t_res.name, t_res.dsem, t_res.dcnt))

    def use_mods(self, layer, m0, gidx, gate_scale, bias_ap=None):
        s = self.s
        mv = self.modall[:, layer, m0 * KC:(m0 + 3) * KC, :]
        self.shift = mv[:, 0:KC, :]
        scale = mv[:, KC:2 * KC, :]
        gate = mv[:, 2 * KC:3 * KC, :]
        g = self.vcol("norm_g", layer)[:, gidx * KC:(gidx + 1) * KC]
        for col in range(2):
            s.op("dve", lambda e, col=col: e.scalar_tensor_tensor(
                out=self.mA[:, :, col], in0=scale[:, :, col], scalar=1.0, in1=g,
                op0=ALU.add, op1=ALU.mult), reads=[self.t_modall, self.t_vec], writes=[self.t_mAG])
        s.op("dve", lambda e: e.tensor_scalar(self.mG[:], gate, float(gate_scale), None, op0=ALU.mult),
             reads=[self.t_modall], writes=[self.t_mAG])
        if bias_ap is not None:
            for col in range(2):
                s.op("dve", lambda e, col=col: e.tensor_tensor(out=self.mB[:, :, col], in0=self.mG[:, :, col],
                                                               in1=bias_ap, op=ALU.mult),
                     reads=[self.t_vec, self.t_mAG], writes=[self.t_mAG])

    def emit_norm_mod(self, src, t_src, o, n, kind, hT_off, t_h):
        s = self.s
        ps, t_ps = self.psum[6], self.t_ps[6]
        for c in range(KC):
            b = c % 2
            s.op("act", lambda e, c=c, b=b: e.activation(out=self.sq[b][:, :n], in_=src[:, c, o:o + n],
                                                         func=AF.Square),
                 reads=[t_src], writes=[self.t_sq[b]])
            s.op("pe", lambda e, c=c, b=b: e.matmul(ps[:, :n], self.ones[:], self.sq[b][:, :n],
                                                    start=(c == 0), stop=(c == KC - 1)),
                 reads=[self.t_sq[b], self.t_ones], writes=[t_ps])
        s.op("act", lambda e: e.activation(out=self.rstd[:, :n], in_=ps[:, :n], func=AF.Sqrt,
                                           scale=1.0 / D, bias=self.eps_rms[:]),
             reads=[t_ps, self.t_eps], writes=[self.t_rstd])
        s.op("dve", lambda e: e.reciprocal(self.rstd[:, :n], self.rstd[:, :n]),
             reads=[self.t_rstd], writes=[self.t_rstd])
        for c in range(KC):
            b = c % 2
            s.op("dve", lambda e, c=c, b=b: e.scalar_tensor_tensor(
                out=self.tmp[b][:, :n], in0=src[:, c, o:o + n], scalar=self.mA[:, c, kind:kind + 1],
                in1=self.rstd[:, :n], op0=ALU.mult, op1=ALU.mult),
                reads=[t_src, self.t_rstd, self.t_mAG], writes=[self.t_tmp[b]])
            s.op("act", lambda e, c=c, b=b: e.activation(
                out=self.hT[:, c, hT_off:hT_off + n], in_=self.tmp[b][:, :n], func=AF.Identity,
                bias=self.shift[:, c, kind:kind + 1]),
                reads=[self.t_tmp[b], self.t_modall], writes=[t_h])

    def emit_ffn(self, layer, half, with_ctx=True):
        s = self.s
        d_wg = self.din("wg_%d_%d" % (layer, half), [D, FF])
        d_wu = self.din("wu_%d_%d" % (layer, half), [D, FF])
        d_wd = self.din("wd_%d_%d" % (layer, half), [FF, D])
        xtiles = self.xtiles if with_ctx else self.xtiles[:4]
        self.use_mods(layer, 6 * half, 2 * half, 0.5)
        for i, (o, n, kind) in enumerate(xtiles):
            self.emit_norm_mod(self.xT, self.t_x[i], o, n, kind, o, self.t_h[i])
        with ExitStack() as es:
            wg = [self.sb("wg", [128, KC, 512], BF16, es) for i in range(2)]
            wu = [self.sb("wu", [128, KC, 512], BF16, es) for i in range(2)]
            wd = [self.sb("wd", [128, 4, D], BF16, es) for i in range(2)]
            act = [self.sb("act", [128, 4, 512], BF16, es) for i in range(2)]
            sg = [self.sb("sg", [128, 512], F32, es) for i in range(2)]
            t_wg = [self.tile("wg") for _ in range(2)]
            t_wu = [self.tile("wu") for _ in range(2)]
            t_wd = [self.tile("wd") for _ in range(2)]
            t_act = [self.tile("act") for _ in range(2)]
            t_sg = [self.tile("sg") for _ in range(2)]
            srcg = d_wg.rearrange("(k p) f -> p k f", p=128)
            srcu = d_wu.rearrange("(k p) f -> p k f", p=128)
            srcd = d_wd.rearrange("(j p) d -> p j d", p=128)
            groups = [(g * 4, min(4, FC - g * 4)) for g in range((FC + 3) // 4)]
            st = {"gu": 0, "y": 0, "act": 0}

            def load(gi):
                f0, nf = groups[gi]
                b = gi % 2
                s.dma("pool", wg[b][:, :, :nf * 128], srcg[:, :, f0 * 128:(f0 + nf) * 128], writes=[t_wg[b]])
                s.dma("pool", wu[b][:, :, :nf * 128], srcu[:, :, f0 * 128:(f0 + nf) * 128], writes=[t_wu[b]])
                s.dma("pool", wd[b][:, :nf, :], srcd[:, f0:f0 + nf, :], writes=[t_wd[b]])

            load(0)
            for gi, (f0, nf) in enumerate(groups):
                b = gi % 2
                if gi + 1 < len(groups):
                    load(gi + 1)

                def gate_up(ti, ab):
                    o, n, kind = xtiles[ti]
                    for j in range(nf):
                        pb = (st["gu"] % 2) * 2
                        st["gu"] += 1
                        pg, pu = self.psum[pb], self.psum[pb + 1]
                        tg, tu = self.t_ps[pb], self.t_ps[pb + 1]
                        for k in range(KC):
                            s.op("pe", lambda e, k=k, j=j: e.matmul(
                                pg[:, :n], wg[b][:, k, j * 128:(j + 1) * 128], self.hT[:, k, o:o + n],
                                start=(k == 0), stop=(k == KC - 1)),
                                reads=[t_wg[b], self.t_h[ti]], writes=[tg], signal=(k == KC - 1))
                        for k in range(KC):
                            s.op("pe", lambda e, k=k, j=j: e.matmul(
                                pu[:, :n], wu[b][:, k, j * 128:(j + 1) * 128], self.hT[:, k, o:o + n],
                                start=(k == 0), stop=(k == KC - 1)),
                                reads=[t_wu[b], self.t_h[ti]], writes=[tu], signal=(k == KC - 1))
                        sb_ = j % 2
                        s.op("act", lambda e, sb_=sb_: e.activation(out=sg[sb_][:, :n], in_=pg[:, :n], func=AF.Silu),
                             reads=[tg], writes=[t_sg[sb_]])
                        s.op("dve", lambda e, sb_=sb_, j=j: e.tensor_tensor(
                            out=act[ab][:, j, :n], in0=pu[:, :n], in1=sg[sb_][:, :n], op=ALU.mult),
                            reads=[tu, t_sg[sb_]], writes=[t_act[ab]])

                def down(ti, ab):
                    o, n, kind = xtiles[ti]
                    for dc in range(KC):
                        pb = 4 + (st["y"] % 2)
                        st["y"] += 1
                        py, ty = self.psum[pb], self.t_ps[pb]
                        for j in range(nf):
                            s.op("pe", lambda e, j=j, dc=dc: e.matmul(
                                py[:, :n], wd[b][:, j, dc * 128:(dc + 1) * 128], act[ab][:, j, :n],
                                start=(j == 0), stop=(j == nf - 1)),
                                reads=[t_wd[b], t_act[ab]], writes=[ty], signal=(j == nf - 1))
                        s.op("dve", lambda e, dc=dc: e.scalar_tensor_tensor(
                            out=self.xT[:, dc, o:o + n], in0=py[:, :n], scalar=self.mG[:, dc, kind:kind + 1],
                            in1=self.xT[:, dc, o:o + n], op0=ALU.mult, op1=ALU.add),
                            reads=[ty, self.t_mAG], writes=[self.t_x[ti]])

                nt = len(xtiles)
                abs_ = []
                for ti in range(nt + 1):
                    if ti < nt:
                        abs_.append(st["act"] % 2)
                        st["act"] += 1
                        gate_up(ti, abs_[ti])
                    if ti >= 1:
                        down(ti - 1, abs_[ti - 1])
            self.stage_barrier()

    def load_halo(self, xh, t_xh):
        d_halo = self.din("haloT", [128, KC, NHALO])
        self.s.dma("sp", xh[:], d_halo, writes=[t_xh])

    def emit_mix_conv(self, layer, jl):
        s = self.s
        d_w1 = self.din("w1_%d" % jl, [D, 2 * D])
        d_w2 = self.din("w2_%d" % jl, [D, D])
        d_cmask = self.din("cmask", [128, 2])
        b1 = self.vcol("conv_b1", jl)
        wdw = self.vcol("conv_wdw", jl)
        bdw = self.vcol("conv_bdw", jl)
        lng = self.vcol("conv_lng", jl)
        lnb = self.vcol("conv_lnb", jl)
        b2 = self.vcol("conv_b2", jl)
        self.use_mods(layer, 3, 1, 1.0, bias_ap=b2)
        with ExitStack() as es:
            cmask = self.sb("cmask", [128, 2], F32, es)
            hb1 = self.sb("hb1", [128, 16], F32, es)
            es_u = ExitStack()
            ubuf = self.sb("ubuf", [128, KC, UL], BF16, es_u)
            t_u = self.tile("u")
            s.op("pool", lambda e: e.memset(ubuf[:], 0.0), writes=[t_u])
            t_small = self.tile("small")
            s.dma("sp", cmask[:], d_cmask, writes=[t_small])
            s.op("dve", lambda e: e.tensor_scalar(hb1[:], b1, 0.5, None, op0=ALU.mult),
                 reads=[self.t_vec], writes=[t_small])
            with ExitStack() as es2:
                xh = self.sb("xhalo", [128, KC, NHALO], F32, es2)
                t_xh = self.tile("xh")
                self.load_halo(xh, t_xh)
                for i, (o, n, kind) in enumerate(self.xtiles):
                    self.emit_norm_mod(self.xT, self.t_x[i], o, n, kind, o, self.t_h[i])
                self.emit_norm_mod(xh, t_xh, 240, 32, 0, NX + 240, self.t_h[5])
                self.stage_barrier()
            es_a = ExitStack()
            w1 = [self.sb("w1", [128, KC, 2, 128], BF16, es_a) for i in range(2)]
            t_w1 = [self.tile("w1") for _ in range(2)]
            sg = [self.sb("sgc", [128, 512], F32, es_a) for i in range(2)]
            t_sg = [self.tile("sgc") for _ in range(2)]
            src1 = d_w1.rearrange("(k p) (two c f) -> p k two c f", p=128, two=2, c=KC)
            tsets = [(o, 512, 16 + o, None, self.t_h[i]) for i, (o, _, _) in enumerate(self.xtiles[:4])]
            tsets.append((NOWN, NCTX, CB + 16, None, self.t_h[4]))
            tsets.append((NX + 240, 16, 0, 0, self.t_h[5]))
            tsets.append((NX + 256, 16, 16 + NOWN, 1, self.t_h[5]))
            cnt = 0
            for c in range(KC):
                b = c % 2
                for half in range(2):
                    s.dma("pool", w1[b][:, :, half, :], src1[:, :, half, c, :], writes=[t_w1[b]])
                for (ho, n, uo, mk, th) in tsets:
                    pb = (cnt % 2) * 2
                    sb_ = cnt % 2
                    cnt += 1
                    pa, pg = self.psum[pb], self.psum[pb + 1]
                    ta, tg = self.t_ps[pb], self.t_ps[pb + 1]
                    for half, pp, tp in ((0, pa, ta), (1, pg, tg)):
                        for k in range(KC):
                            s.op("pe", lambda e, k=k, half=half, pp=pp: e.matmul(
                                pp[:, :n], w1[b][:, k, half, :], self.hT[:, k, ho:ho + n],
                                start=(k == 0), stop=(k == KC - 1)),
                                reads=[t_w1[b], th], writes=[tp], signal=(k == KC - 1))
                    s.op("act", lambda e, sb_=sb_, pg=pg: e.activation(
                        out=sg[sb_][:, :n], in_=pg[:, :n], func=AF.Tanh, scale=0.5,
                        bias=hb1[:, KC + c:KC + c + 1]), reads=[tg, t_small], writes=[t_sg[sb_]])
                    s.op("act", lambda e, sb_=sb_, pa=pa: e.activation(
                        out=self.tmp[sb_][:, :n], in_=pa[:, :n], func=AF.Identity, scale=0.5,
                        bias=hb1[:, c:c + 1]), reads=[ta, t_small], writes=[self.t_tmp[sb_]])
                    s.op("dve", lambda e, sb_=sb_: e.scalar_tensor_tensor(
                        out=ubuf[:, c, uo:uo + n], in0=sg[sb_][:, :n], scalar=1.0, in1=self.tmp[sb_][:, :n],
                        op0=ALU.add, op1=ALU.mult), reads=[t_sg[sb_], self.t_tmp[sb_]], writes=[t_u])
                    if mk is not None:
                        s.op("dve", lambda e, mk=mk: e.tensor_scalar(
                            ubuf[:, c, uo:uo + n], ubuf[:, c, uo:uo + n], cmask[:, mk:mk + 1], None,
                            op0=ALU.mult), reads=[t_small, t_u], writes=[t_u])
            self.stage_barrier()
            es_a.close()
            es_b = ExitStack()
            dg = [self.sb("dg", [128, CONV_W, 128], BF16, es_b) for i in range(2)]
            t_dg = [self.tile("dg") for _ in range(2)]
            otiles = [(o, 512, o + 1, self.t_h[i]) for i, (o, _, _) in enumerate(self.xtiles[:4])]
            otiles.append((NOWN, NCTX, CB + 1, self.t_h[4]))
            cnt = 0
            for c in range(KC):
                b = c % 2
                for k in range(CONV_W):
                    s.op("pool", lambda e, k=k: e.tensor_scalar(
                        dg[b][:, k, :], self.ident[:], wdw[:, c * CONV_W + k:c * CONV_W + k + 1], None,
                        op0=ALU.mult), reads=[self.t_vec, self.t_ones], writes=[t_dg[b]],
                        signal=(k == CONV_W - 1))
                for (o, n, ub, th) in otiles:
                    pb = 4 + cnt % 2
                    cnt += 1
                    pc, tpc = self.psum[pb], self.t_ps[pb]
                    for k in range(CONV_W):
                        s.op("pe", lambda e, k=k: e.matmul(
                            pc[:, :n], dg[b][:, k, :], ubuf[:, c, ub + k:ub + k + n],
                            start=(k == 0), stop=(k == CONV_W - 1)),
                            reads=[t_dg[b], t_u], writes=[tpc], signal=(k == CONV_W - 1))
                    s.op("act", lambda e: e.activation(
                        out=self.hT[:, c, o:o + n], in_=pc[:, :n], func=AF.Identity, bias=bdw[:, c:c + 1]),
                        reads=[tpc, self.t_vec], writes=[th])
            self.stage_barrier()
            es_b.close()
            es_u.close()
            lnm = self.sb("lnm", [128, 512], F32, es)
            lnv = self.sb("lnv", [128, 512], F32, es)
            t_lnm, t_lnv = self.tile("lnm"), self.tile("lnv")
            for i, (o, n, kind) in enumerate(self.xtiles):
                th = self.t_h[i]
                p1, t1 = self.psum[6], self.t_ps[6]
                p2, t2 = self.psum[7], self.t_ps[7]
                for c in range(KC):
                    b = c % 2
                    s.op("pe", lambda e, c=c: e.matmul(p1[:, :n], self.ones[:], self.hT[:, c, o:o + n],
                                                       start=(c == 0), stop=(c == KC - 1)),
                         reads=[th, self.t_ones], writes=[t1], signal=(c == KC - 1))
                    s.op("act", lambda e, c=c, b=b: e.activation(out=self.sq[b][:, :n], in_=self.hT[:, c, o:o + n],
                                                                 func=AF.Square),
                         reads=[th], writes=[self.t_sq[b]])
                    s.op("pe", lambda e, c=c, b=b: e.matmul(p2[:, :n], self.ones[:], self.sq[b][:, :n],
                                                            start=(c == 0), stop=(c == KC - 1)),
                         reads=[self.t_sq[b], self.t_ones], writes=[t2])
                s.op("dve", lambda e: e.tensor_scalar(lnm[:, :n], p1[:, :n], 1.0 / D, None, op0=ALU.mult),
                     reads=[t1], writes=[t_lnm])
                s.op("dve", lambda e: e.tensor_tensor(out=lnv[:, :n], in0=lnm[:, :n], in1=lnm[:, :n], op=ALU.mult),
                     reads=[t_lnm], writes=[t_lnv])
                s.op("dve", lambda e: e.scalar_tensor_tensor(
                    out=lnv[:, :n], in0=p2[:, :n], scalar=1.0 / D, in1=lnv[:, :n],
                    op0=ALU.mult, op1=ALU.subtract), reads=[t2, t_lnv], writes=[t_lnv])
                s.op("act", lambda e: e.activation(out=self.rstd[:, :n], in_=lnv[:, :n], func=AF.Sqrt,
                                                   bias=self.eps_ln[:]),
                     reads=[t_lnv, self.t_eps], writes=[self.t_rstd])
                s.op("dve", lambda e: e.reciprocal(self.rstd[:, :n], self.rstd[:, :n]),
                     reads=[self.t_rstd], writes=[self.t_rstd])
                for c in range(KC):
                    b = c % 2
                    s.op("dve", lambda e, c=c, b=b: e.tensor_tensor(
                        out=self.tmp[b][:, :n], in0=self.hT[:, c, o:o + n], in1=lnm[:, :n], op=ALU.subtract),
                        reads=[th, t_lnm], writes=[self.t_tmp[b]])
                    s.op("dve", lambda e, c=c, b=b: e.scalar_tensor_tensor(
                        out=self.tmp[b][:, :n], in0=self.tmp[b][:, :n], scalar=lng[:, c:c + 1],
                        in1=self.rstd[:, :n], op0=ALU.mult, op1=ALU.mult),
                        reads=[self.t_tmp[b], self.t_rstd, self.t_vec], writes=[self.t_tmp[b]])
                    s.op("act", lambda e, c=c, b=b: e.activation(
                        out=self.hT[:, c, o:o + n], in_=self.tmp[b][:, :n], func=AF.Silu, bias=lnb[:, c:c + 1]),
                        reads=[self.t_tmp[b], self.t_vec], writes=[th])
            w2 = self.sb("w2", [128, KC, D], BF16, es)
            t_w2 = self.tile("w2")
            s.dma("pool", w2[:], d_w2.rearrange("(k p) f -> p k f", p=128), writes=[t_w2])
            self.emit_out_proj(w2, t_w2, self.hT, self.t_h, self.xtiles)
            self.stage_barrier()

    def emit_out_proj(self, w, t_w, src, t_src, xtiles):
        s = self.s
        cnt = 0
        for i, (o, n, kind) in enumerate(xtiles):
            for dc in range(KC):
                pb = cnt % 4
                b = cnt % 2
                cnt += 1
                py, ty = self.psum[pb], self.t_ps[pb]
                for k in range(KC):
                    s.op("pe", lambda e, k=k, dc=dc: e.matmul(
                        py[:, :n], w[:, k, dc * 128:(dc + 1) * 128], src[:, k, o:o + n],
                        start=(k == 0), stop=(k == KC - 1)),
                        reads=[t_w, t_src[i]], writes=[ty], signal=(k == KC - 1))
                s.op("act", lambda e, dc=dc, b=b: e.activation(
                    out=self.tmp[b][:, :n], in_=py[:, :n], func=AF.Identity,
                    scale=self.mG[:, dc, kind:kind + 1], bias=self.mB[:, dc, kind:kind + 1]),
                    reads=[ty, self.t_mAG], writes=[self.t_tmp[b]])
                s.op("dve", lambda e, dc=dc, b=b: e.tensor_tensor(
                    out=self.xT[:, dc, o:o + n], in0=self.xT[:, dc, o:o + n], in1=self.tmp[b][:, :n], op=ALU.add),
                    reads=[self.t_tmp[b]], writes=[self.t_x[i]])

    def emit_mix_na(self, layer, jl, ctx_out):
        s = self.s
        d_wqkv = self.din("wqkv_%d" % jl, [D, 3 * D])
        d_wo = self.din("wo_%d" % jl, [D, D])
        d_tab = self.din("tab_%d" % jl, [N_HEADS, 128, 25, 128])
        d_bvb = self.din("bvb_%d" % jl, [128, D])
        bqkv = self.vcol("na_bqkv", jl)
        bo = self.vcol("na_bo", jl)
        self.use_mods(layer, 3, 1, 1.0, bias_ap=bo)
        xtiles = self.xtiles if ctx_out else self.xtiles[:4]
        with ExitStack() as es:
            with ExitStack() as es2:
                xh = self.sb("xhalo", [128, KC, NHALO], F32, es2)
                t_xh = self.tile("xh")
                self.load_halo(xh, t_xh)
                for i, (o, n, kind) in enumerate(self.xtiles):
                    self.emit_norm_mod(self.xT, self.t_x[i], o, n, kind, o, self.t_h[i])
                self.emit_norm_mod(xh, t_xh, 0, NHALO, 0, NX, self.t_h[5])
                self.stage_barrier()
            oT = self.sb("oT", [128, KC, NX], BF16, es)
            t_o = [self.tile("o") for _ in range(5)]
            with ExitStack() as es3:
                wq = self.sb("wqkv", [128, KC, 3, 128], BF16, es3)
                t_wq = self.tile("wqkv")
                qT = self.sb("qT", [128, NX], BF16, es3)
                kT = self.sb("kT", [128, NEXT], BF16, es3)
                V = self.sb("V", [128, 22, 128], BF16, es3)
                t_q, t_k, t_v = self.tile("q"), self.tile("k"), self.tile("v")
                tab = [self.sb("tab", [128, 2, 5, 128], BF16, es3) for i in range(2)]
                t_tab = [self.tile("tab") for _ in range(2)]
                P = [self.sb("P", [128, 7 * 128], BF16, es3) for i in range(2)]
                t_P = [self.tile("P") for _ in range(2)]
                bvb = self.sb("bvb", [128, D], F32, es3)
                t_bvb = self.tile("bvb")
                rden = self.sb("rden", [128, 128], F32, es3)
                t_rden = self.tile("rden")
                s.dma("sp", bvb[:], d_bvb, writes=[t_bvb])
                srcw = d_wqkv.rearrange("(k p) (three c f) -> p k three c f", p=128, three=3, c=KC)
                ksegs = [(NX, 256, 0, self.t_h[5])]
                ksegs += [(o, 512, 256 + o, self.t_h[i]) for i, (o, _, _) in enumerate(self.xtiles[:4])]
                ksegs += [(NX + 256, 256, 256 + NOWN, self.t_h[5]), (NOWN, NCTX, 512 + NOWN, self.t_h[4])]

                def kpos(Tt):
                    if Tt < 2:
                        return NX + Tt * 128, self.t_h[5]
                    if Tt < 18:
                        return (Tt - 2) * 128, self.t_h[(Tt - 2) // 4]
                    if Tt < 20:
                        return NX + 256 + (Tt - 18) * 128, self.t_h[5]
                    return NOWN + (Tt - 20) * 128, self.t_h[4]

                order = [(0, 1), (1, 2)] + [(p, 0) for p in range(2, 14)] + [(14, 3), (15, 4)]
                cn = {"tab": 0, "s": 0, "o": 0, "p": 0}
                for j in range(KC):
                    for three in range(3):
                        s.dma("pool", wq[:, :, three, :], srcw[:, :, three, j, :], writes=[t_wq])
                    for i, (o, n, kind) in enumerate(xtiles):
                        pb = 6 + cn["p"] % 2
                        cn["p"] += 1
                        pp, tp = self.psum[pb], self.t_ps[pb]
                        for k in range(KC):
                            s.op("pe", lambda e, k=k: e.matmul(pp[:, :n], wq[:, k, 0, :], self.hT[:, k, o:o + n],
                                                               start=(k == 0), stop=(k == KC - 1)),
                                 reads=[t_wq, self.t_h[i]], writes=[tp], signal=(k == KC - 1))
                        s.op("dve", lambda e: e.tensor_scalar(qT[:, o:o + n], pp[:, :n], bqkv[:, j:j + 1], 0.125,
                                                              op0=ALU.add, op1=ALU.mult),
                             reads=[tp, self.t_vec], writes=[t_q])
                    for (ho, n, ko, th) in ksegs:
                        pb = 6 + cn["p"] % 2
                        cn["p"] += 1
                        pp, tp = self.psum[pb], self.t_ps[pb]
                        for k in range(KC):
                            s.op("pe", lambda e, k=k: e.matmul(pp[:, :n], wq[:, k, 1, :], self.hT[:, k, ho:ho + n],
                                                               start=(k == 0), stop=(k == KC - 1)),
                                 reads=[t_wq, th], writes=[tp], signal=(k == KC - 1))
                        s.op("act", lambda e: e.activation(out=kT[:, ko:ko + n], in_=pp[:, :n], func=AF.Identity,
                                                           bias=bqkv[:, KC + j:KC + j + 1]),
                             reads=[tp, self.t_vec], writes=[t_k])
                    for T0 in range(0, 22, 4):
                        nT = min(4, 22 - T0)
                        pb = 6 + cn["p"] % 2
                        cn["p"] += 1
                        pp, tp = self.psum[pb], self.t_ps[pb]
                        for q in range(nT):
                            hp, th = kpos(T0 + q)
                            for k in range(KC):
                                s.op("pe", lambda e, k=k, q=q, hp=hp: e.matmul(
                                    pp[:, q * 128:(q + 1) * 128], self.hT[:, k, hp:hp + 128], wq[:, k, 2, :],
                                    start=(k == 0), stop=(k == KC - 1)),
                                    reads=[t_wq, th], writes=[tp], signal=(k == KC - 1 and q == nT - 1))
                        for q in range(nT):
                            s.op("dve", lambda e, q=q: e.tensor_tensor(
                                out=V[:, T0 + q, :], in0=pp[:, q * 128:(q + 1) * 128],
                                in1=bvb[:, j * 128:(j + 1) * 128], op=ALU.add),
                                reads=[tp, t_bvb], writes=[t_v])
                    blocks = []
                    for (p, var) in order:
                        keys = [(p + t, t) for t in range(5)] + [(20, None), (21, None)]
                        blocks.append((p * 128, keys, var, p // 4))
                    if ctx_out:
                        for cb_ in range(2):
                            blocks.append((NOWN + cb_ * 128, [(20, None), (21, None)], None, 4))
                    cur_var = None
                    tb = None
                    for (qo, keys, var, oti) in blocks:
                        if var is not None and var != cur_var:
                            tb = cn["tab"] % 2
                            cn["tab"] += 1
                            for hh_ in range(2):
                                s.dma("pool", tab[tb][:, hh_, :, :],
                                      d_tab[2 * j + hh_, :, var * 5:(var + 1) * 5, :], writes=[t_tab[tb]])
                            cur_var = var
                        ob = 4 + cn["o"] % 2
                        cn["o"] += 1
                        po, t_po = self.psum[ob], self.t_ps[ob]
                        nk = len(keys)
                        for hh in range(2):
                            sa = (cn["s"] % 2) * 2
                            pbk = cn["s"] % 2
                            cn["s"] += 1
                            banks = [(self.psum[sa], self.t_ps[sa]), (self.psum[sa + 1], self.t_ps[sa + 1])]
                            r0 = 64 * hh
                            for t, (Tt, ts) in enumerate(keys):
                                pS, tS = banks[t // 4]
                                col = (t % 4) * 128
                                last_in_bank = (t == nk - 1) or (t % 4 == 3)
                                s.op("pe", lambda e, Tt=Tt, pS=pS, col=col, ts=ts: e.matmul(
                                    pS[:, col:col + 128], kT[r0:r0 + 64, Tt * 128:(Tt + 1) * 128],
                                    qT[r0:r0 + 64, qo:qo + 128], start=True, stop=(ts is None),
                                    tile_position=(r0, 0)),
                                    reads=[t_k, t_q], writes=[tS], signal=(ts is None and last_in_bank))
                                if ts is not None:
                                    s.op("pe", lambda e, pS=pS, col=col, ts=ts: e.matmul(
                                        pS[:, col:col + 128], self.ident[:], tab[tb][:, hh, ts, :],
                                        start=False, stop=True),
                                        reads=[t_tab[tb], self.t_ones], writes=[tS], signal=last_in_bank)
                            n0 = min(nk, 4) * 128
                            s.op("act", lambda e, pbk=pbk, n0=n0: e.activation(
                                out=P[pbk][:, 0:n0], in_=banks[0][0][:, 0:n0], func=AF.Exp),
                                reads=[banks[0][1]], writes=[t_P[pbk]])
                            if nk > 4:
                                n1 = (nk - 4) * 128
                                s.op("act", lambda e, pbk=pbk, n1=n1: e.activation(
                                    out=P[pbk][:, 512:512 + n1], in_=banks[1][0][:, 0:n1], func=AF.Exp),
                                    reads=[banks[1][1]], writes=[t_P[pbk]])
                            for t, (Tt, ts) in enumerate(keys):
                                s.op("pe", lambda e, Tt=Tt, t=t, pbk=pbk: e.matmul(
                                    po[r0:r0 + 64, 0:128], V[:, Tt, r0:r0 + 64], P[pbk][:, t * 128:(t + 1) * 128],
                                    start=(t == 0), stop=(t == nk - 1), tile_position=(0, r0)),
                                    reads=[t_v, t_P[pbk]], writes=[t_po], signal=False)
                            for t, (Tt, ts) in enumerate(keys):
                                s.op("pe", lambda e, t=t, pbk=pbk: e.matmul(
                                    po[r0:r0 + 64, 128:256], self.ones[:, 0:64], P[pbk][:, t * 128:(t + 1) * 128],
                                    start=(t == 0), stop=(t == nk - 1), tile_position=(0, r0)),
                                    reads=[self.t_ones, t_P[pbk]], writes=[t_po], signal=(t == nk - 1))
                        s.op("dve", lambda e: e.reciprocal(rden[:], po[:, 128:256]), reads=[t_po], writes=[t_rden])
                        s.op("dve", lambda e: e.tensor_tensor(out=oT[:, j, qo:qo + 128], in0=po[:, 0:128],
                                                              in1=rden[:], op=ALU.mult),
                             reads=[t_po, t_rden], writes=[t_o[oti]])
                self.stage_barrier()
            wo = self.sb("wo", [128, KC, D], BF16, es)
            t_wo = self.tile("wo")
            s.dma("pool", wo[:], d_wo.rearrange("(k p) f -> p k f", p=128), writes=[t_wo])
            self.emit_out_proj(wo, t_wo, oT, t_o, xtiles)
            self.stage_barrier()

    def emit_final(self):
        s = self.s
        fg = self.vcol("final_g", 0)
        ps, t_ps = self.psum[6], self.t_ps[6]
        for i, (o, n, kind) in enumerate(self.xtiles[:4]):
            for c in range(KC):
                b = c % 2
                s.op("act", lambda e, c=c, b=b: e.activation(out=self.sq[b][:, :n], in_=self.xT[:, c, o:o + n],
                                                             func=AF.Square),
                     reads=[self.t_x[i]], writes=[self.t_sq[b]])
                s.op("pe", lambda e, c=c, b=b: e.matmul(ps[:, :n], self.ones[:], self.sq[b][:, :n],
                                                        start=(c == 0), stop=(c == KC - 1)),
                     reads=[self.t_sq[b], self.t_ones], writes=[t_ps])
            s.op("act", lambda e: e.activation(out=self.rstd[:, :n], in_=ps[:, :n], func=AF.Sqrt,
                                               scale=1.0 / D, bias=self.eps_rms[:]),
                 reads=[t_ps, self.t_eps], writes=[self.t_rstd])
            s.op("dve", lambda e: e.reciprocal(self.rstd[:, :n], self.rstd[:, :n]),
                 reads=[self.t_rstd], writes=[self.t_rstd])
            for c in range(KC):
                s.op("dve", lambda e, c=c: e.scalar_tensor_tensor(
                    out=self.xT[:, c, o:o + n], in0=self.xT[:, c, o:o + n], scalar=fg[:, c:c + 1],
                    in1=self.rstd[:, :n], op0=ALU.mult, op1=ALU.mult),
                    reads=[self.t_rstd, self.t_vec], writes=[self.t_x[i]])

    def build(self):
        s = self.s
        if self.mods_only:
            self.emit_mods_shard()
            self.es.close()
            return self.nc
        self.setup()
        for st in self.stages:
            kind = st[0]
            if kind == "ffn":
                self.emit_ffn(st[1], st[2], st[3])
            elif kind == "conv":
                self.emit_mix_conv(st[1], st[2])
            elif kind == "na":
                self.emit_mix_na(st[1], st[2], st[3])
            else:
                raise ValueError(kind)
        if self.final_norm:
            self.emit_final()
        for i, (o, n, _) in enumerate(self.xtiles):
            s.dma("sp", self.d_out[:, :, o:o + n], self.xT[:, :, o:o + n], reads=[self.t_x[i]],
                  sem_tile=self.t_out)
        s._wait("sp", ("dma_" + self.t_out.name, self.t_out.dsem, self.t_out.dcnt))
        self.es.close()
        return self.nc


def pack_vecs(inp):
    cols = []
    for l in range(DEPTH):
        cols.append(fm(inp["norm_g"][l].reshape(-1)))
    cols.append(fm(inp["final_g"]))
    for l in range(2):
        cols.append(fm(inp["conv_b_pw1"][l]))
    for l in range(2):
        w = inp["conv_w_dw"][l]
        cols.append(np.ascontiguousarray(w.reshape(CONV_W, KC, 128).transpose(2, 1, 0).reshape(128, KC * CONV_W)))
    for key in ("conv_b_dw", "conv_ln_g", "conv_ln_b", "conv_b_pw2"):
        for l in range(2):
            cols.append(fm(inp[key][l]))
    for l in range(2):
        cols.append(fm(inp["na_b_qkv"][l]))
    for l in range(2):
        cols.append(fm(inp["na_b_o"][l]))
    v = np.concatenate(cols, axis=1).astype(np.float32)
    assert v.shape[1] == NVEC, (v.shape, NVEC)
    return np.ascontiguousarray(v)


def build_tables(rpb, core):
    kk = np.arange(128)
    kr, m = kk // 64, kk % 64
    qr, c = kk // 64, kk % 64
    qs = np.clip(c - 8, 0, 48)
    colvalid = (m[:, None] >= qs[None, :]) & (m[:, None] < qs[None, :] + 16)
    relc = np.clip(m[:, None] - c[None, :] + 15, 0, 30)
    out = np.full((N_HEADS, 128, 25, 128), NEG, np.float32)
    for v in range(5):
        p = {0: 5, 1: 0, 2: 1, 3: 14, 4: 15}[v]
        for t in range(5):
            lk = 2 * p - 4 + 2 * t + kr
            lr = 2 * p + qr
            dr = lk[:, None] - lr[None, :]
            valid = colvalid & (dr >= -4) & (dr <= 3)
            rel = dr + 7
            if core == 0 and v in (1, 2):
                rel = np.where(lk[:, None] < 0, dr + 15, rel)
            if core == NCORES - 1 and v in (3, 4):
                rel = np.where(lk[:, None] >= 32, dr - 1, rel)
            rel = np.clip(rel, 0, 14)
            vals = rpb[:, rel, relc]
            out[:, :, v * 5 + t, :] = np.where(valid[None], vals, np.float32(NEG))
    return out


def to_fm_tokens(x):
    n = x.shape[0]
    return np.ascontiguousarray(x.reshape(n, KC, 128).transpose(2, 1, 0))


def from_fm_tokens(xT):
    n = xT.shape[2]
    return np.ascontiguousarray(xT.transpose(2, 1, 0).reshape(n, D))


def make_halos(x_cores):
    hal = []
    for i in range(NCORES):
        top = x_cores[i - 1][1792:2048] if i > 0 else x_cores[0][256:512]
        bot = x_cores[i + 1][0:256] if i < NCORES - 1 else x_cores[i][1536:1792]
        hal.append(np.concatenate([top, bot], axis=0))
    return hal


def run_mods(inp):
    b = Builder([], mods_only=True)
    nc = b.build()
    cs = np.ascontiguousarray(np.stack([fm(inp["c"][0]), fm(inp["c_ctx"])], axis=-1).astype(np.float32))
    in_maps = []
    for i in range(NCORES):
        sl = slice(i * MODSH, (i + 1) * MODSH)
        mb = np.stack([fm(inp["mod_b"][l][sl]) for l in range(DEPTH)], axis=1)
        in_maps.append({"modw_sh": np.ascontiguousarray(inp["mod_w"][:, :, sl]),
                        "modb_sh": np.ascontiguousarray(mb.astype(np.float32)), "cs": cs})
    res = run_bass_kernel_spmd(nc, in_maps, core_ids=list(range(NCORES)))
    return np.ascontiguousarray(np.concatenate([r["mods_sh"] for r in res.results], axis=2))


def run_stages(stages, final_norm, x_cores, xc, halos, inp, mods):
    b = Builder(stages, final_norm)
    nc = b.build()
    shared = {"vecs": pack_vecs(inp), "modsT": mods, "ident": np.eye(128, dtype=np.float32)}
    for name in b.inputs:
        parts = name.split("_")
        if parts[0] in ("wg", "wu", "wd"):
            key = {"wg": "ffn_w_gate", "wu": "ffn_w_up", "wd": "ffn_w_down"}[parts[0]]
            shared[name] = np.ascontiguousarray(inp[key][int(parts[1]), int(parts[2])])
        elif parts[0] in ("w1", "w2", "wqkv", "wo"):
            key = {"w1": "conv_w_pw1", "w2": "conv_w_pw2", "wqkv": "na_w_qkv", "wo": "na_w_o"}[parts[0]]
            shared[name] = np.ascontiguousarray(inp[key][int(parts[1])])
        elif parts[0] == "bvb":
            shared[name] = np.ascontiguousarray(
                np.broadcast_to(inp["na_b_qkv"][int(parts[1])][None, 2 * D:3 * D], (128, D)).astype(np.float32))
    in_maps = []
    for i in range(NCORES):
        m = dict(shared)
        m["xT"] = to_fm_tokens(np.concatenate([x_cores[i], xc], axis=0))
        if "haloT" in b.inputs:
            m["haloT"] = to_fm_tokens(halos[i])
        if "cmask" in b.inputs:
            cm = np.ones((128, 2), np.float32)
            if i == 0:
                cm[:, 0] = 0.0
            if i == NCORES - 1:
                cm[:, 1] = 0.0
            m["cmask"] = cm
        for name in b.inputs:
            if name.startswith("tab_"):
                m[name] = build_tables(inp["na_rpb"][int(name[4:])], i)
        in_maps.append({k: m[k] for k in b.inputs})
    res = run_bass_kernel_spmd(nc, in_maps, core_ids=list(range(NCORES)))
    outs = [from_fm_tokens(r["outT"]) for r in res.results]
    return [o[:NOWN] for o in outs], outs[0][NOWN:]


def kernel(**inp):
    inp = {k: np.asarray(v) for k, v in inp.items()}
    x = inp["x"][0]
    x_cores = [np.ascontiguousarray(x[i * NOWN:(i + 1) * NOWN]) for i in range(NCORES)]
    xc = np.ascontiguousarray(inp["ctx"][0])
    mods = run_mods(inp)
    mix = {0: ("conv", 0, 0), 1: ("na", 1, 0, True), 2: ("conv", 2, 1), 3: ("na", 3, 1, False)}
    x_cores, xc = run_stages([("ffn", 0, 0, True)], False, x_cores, xc, None, inp, mods)
    for l in range(DEPTH):
        last = l == DEPTH - 1
        stages = [mix[l], ("ffn", l, 1, not last)]
        if not last:
            stages.append(("ffn", l + 1, 0, True))
        x_cores, xc = run_stages(stages, last, x_cores, xc, make_halos(x_cores), inp, mods)
    return np.concatenate(x_cores, axis=0)[None].astype(np.float32)
```
